# Optimizing a Trainium2 kernel written in Bass

```python
import jax, jax.numpy as jnp
from jax import lax
import numpy as np

D_MODEL = 1024
BATCH = 2
SEQ = 8192
DEPTH = 2

HEAD_DIM = 64
H_HGRN = 4
H_GLA = 6
H_RWKV = 6
W_HGRN = H_HGRN * HEAD_DIM
W_GLA = H_GLA * HEAD_DIM
W_RWKV = H_RWKV * HEAD_DIM
MIX_WIDTH = W_HGRN + W_GLA + W_RWKV

GLA_GATE_RANK = 16
GLA_GATE_NORMALIZER = 16.0
RWKV_DECAY_RANK = 64
RWKV_ICLR_RANK = 64
RWKV_GATE_RANK = 128
RWKV_GN_EPS = 64e-5

N_HGRN_COLS = 4 * W_HGRN
N_GLA_COLS = 4 * W_GLA + GLA_GATE_RANK
N_RWKV_COLS = 3 * W_RWKV + RWKV_DECAY_RANK + RWKV_ICLR_RANK + RWKV_GATE_RANK
N_IN = N_HGRN_COLS + N_GLA_COLS + N_RWKV_COLS
RWKV_SPLITS = [W_RWKV, 2 * W_RWKV, 3 * W_RWKV, 3 * W_RWKV + RWKV_DECAY_RANK,
               3 * W_RWKV + RWKV_DECAY_RANK + RWKV_ICLR_RANK]

D_FF = 4 * D_MODEL
CHUNK = 64
LN_EPS = 1e-5
RMS_EPS = 1e-5
F_MIN = 1e-30
DEEPNORM_ALPHA = (2.0 * DEPTH) ** 0.25
DEEPNORM_BETA = (8.0 * DEPTH) ** -0.25

kernel_name = "hybrid_hgrn2_gla_rwkv7_deepnorm_adaln"


def split_heads(t, n_heads):
    return t.reshape(t.shape[:-1] + (n_heads, -1))


def layer_norm(x, g, b):
    xf = x.astype(jnp.float32)
    mu = jnp.mean(xf, axis=-1, keepdims=True)
    var = jnp.mean(jnp.square(xf - mu), axis=-1, keepdims=True)
    return ((xf - mu) * lax.rsqrt(var + LN_EPS)).astype(x.dtype) * g + b


def head_rms_norm(x, g, n_heads):
    xh = split_heads(x, n_heads).astype(jnp.float32)
    xh = xh * lax.rsqrt(jnp.mean(xh * xh, axis=-1, keepdims=True) + RMS_EPS)
    return xh.reshape(x.shape).astype(x.dtype) * g


def head_group_norm(x, g, b, n_heads, eps):
    xh = split_heads(x, n_heads).astype(jnp.float32)
    mu = jnp.mean(xh, axis=-1, keepdims=True)
    var = jnp.mean(jnp.square(xh - mu), axis=-1, keepdims=True)
    return ((xh - mu) * lax.rsqrt(var + eps)).reshape(x.shape).astype(x.dtype) * g + b


def chunked_gated_linear_attention(q, k, v, log_g):
    B, T, H, K = q.shape
    V = v.shape[-1]
    n = T // CHUNK

    def to_chunks(a):
        return jnp.moveaxis(a.astype(jnp.float32).reshape(B, n, CHUNK, H, a.shape[-1]), 1, 0)

    causal = jnp.tril(jnp.ones((CHUNK, CHUNK), dtype=bool))[None, :, :, None, None]

    def step(S, inp):
        qc, kc, vc, gc = inp
        b = jnp.cumsum(gc, axis=1)
        diff = b[:, :, None] - b[:, None, :]
        decay = jnp.where(causal, jnp.exp(jnp.minimum(diff, 0.0)), 0.0)
        scores = jnp.einsum('bthk,bshk,btshk->bths', qc, kc, decay)
        o = (jnp.einsum('bths,bshv->bthv', scores, vc)
             + jnp.einsum('bthk,bhkv->bthv', qc * jnp.exp(b), S))
        b_end = b[:, -1]
        S = (S * jnp.exp(b_end)[..., None]
             + jnp.einsum('bshk,bshv->bhkv', kc * jnp.exp(b_end[:, None] - b), vc))
        return S, o

    S0 = jnp.zeros((B, H, K, V), jnp.float32)
    _, o = lax.scan(step, S0, (to_chunks(q), to_chunks(k), to_chunks(v), to_chunks(log_g)))
    return jnp.moveaxis(o, 0, 1).reshape(B, T, H, V)


def rwkv7_recurrence(r, w, k, v, a, b):
    B, T, H, D = r.shape

    def step(S, inp):
        r_t, w_t, k_t, v_t, a_t, b_t = inp
        sa = jnp.einsum('bhvk,bhk->bhv', S, a_t)
        S = S * w_t[:, :, None, :] + sa[..., None] * b_t[:, :, None, :] + v_t[..., None] * k_t[:, :, None, :]
        return S, jnp.einsum('bhvk,bhk->bhv', S, r_t)

    xs = tuple(jnp.moveaxis(t.astype(jnp.float32), 1, 0) for t in (r, w, k, v, a, b))
    S0 = jnp.zeros((B, H, D, D), jnp.float32)
    _, y = lax.scan(step, S0, xs)
    return jnp.moveaxis(y, 0, 1)


def hgrn2_mixer(z, lower_bound, norm_g):
    B, T, _ = z.shape
    q, f_logit, i, gate = jnp.split(z, 4, axis=-1)
    zf = f_logit.astype(jnp.float32)
    f = lower_bound + (1.0 - lower_bound) * jax.nn.sigmoid(zf)
    log_f = jnp.log(jnp.maximum(f, F_MIN))
    k = (1.0 - lower_bound) * jax.nn.sigmoid(-zf)
    q = jax.nn.silu(q.astype(jnp.float32)) * HEAD_DIM ** -0.5
    o = chunked_gated_linear_attention(split_heads(q, H_HGRN), split_heads(k, H_HGRN),
                                       split_heads(i, H_HGRN), split_heads(log_f, H_HGRN))
    o = o.reshape(B, T, W_HGRN).astype(z.dtype)
    return head_rms_norm(o, norm_g, H_HGRN) * jax.nn.silu(gate)


def gla_mixer(z, alpha_up, alpha_b, norm_g):
    B, T, _ = z.shape
    q, k, v, r, h_alpha = jnp.split(z, [W_GLA, 2 * W_GLA, 3 * W_GLA, 4 * W_GLA], axis=-1)
    log_alpha = jax.nn.log_sigmoid((h_alpha @ alpha_up + alpha_b).astype(jnp.float32)) / GLA_GATE_NORMALIZER
    o = chunked_gated_linear_attention(split_heads(q * HEAD_DIM ** -0.5, H_GLA), split_heads(k, H_GLA),
                                       split_heads(v, H_GLA), split_heads(log_alpha, H_GLA))
    o = o.reshape(B, T, W_GLA).astype(z.dtype)
    return head_rms_norm(o, norm_g, H_GLA) * jax.nn.silu(r)


def rwkv7_mixer(z, mu, w0, w_up, a0, a_up, g_up, k_k, k_a, r_k, gn_g, gn_b):
    B, T, _ = z.shape
    z_prev = jnp.pad(z, ((0, 0), (1, 0), (0, 0)))[:, :-1]
    z = z + (z_prev - z) * mu
    r, k, v, h_w, h_a, h_g = jnp.split(z, RWKV_SPLITS, axis=-1)
    w_log = -jax.nn.softplus(-(w0 + jnp.tanh(h_w) @ w_up)) - 0.5
    decay = jnp.exp(-jnp.exp(w_log.astype(jnp.float32)))
    a = jax.nn.sigmoid(a0 + h_a @ a_up)
    g = jax.nn.sigmoid(h_g) @ g_up
    kk = split_heads(k * k_k, H_RWKV).astype(jnp.float32)
    kk = kk / jnp.maximum(jnp.sqrt(jnp.sum(kk * kk, axis=-1, keepdims=True)), 1e-12)
    k = k * (1.0 + (a - 1.0) * k_a)
    a_h = split_heads(a, H_RWKV).astype(jnp.float32)
    r_h, k_h, v_h = split_heads(r, H_RWKV), split_heads(k, H_RWKV), split_heads(v, H_RWKV)
    y = rwkv7_recurrence(r_h, split_heads(decay, H_RWKV), k_h, v_h, -kk, kk * a_h)
    y = head_group_norm(y.reshape(B, T, W_RWKV).astype(z.dtype), gn_g, gn_b, H_RWKV, RWKV_GN_EPS)
    bonus = jnp.sum(r_h * k_h * r_k, axis=-1, keepdims=True) * v_h
    return (y + bonus.reshape(B, T, W_RWKV)) * g


def setup_inputs(seed: int = 0) -> dict:
    key = jax.random.key(seed)
    ks = iter(jax.random.split(key, 32))
    nrm = lambda shape, s: s * jax.random.normal(next(ks), shape, jnp.float32)
    L = DEPTH
    return {
        "x": nrm((BATCH, SEQ, D_MODEL), 1.0),
        "c": nrm((BATCH, D_MODEL), 1.0),
        "hgrn_lb_logits": nrm((L, W_HGRN), 1.0),
        "ada_w": nrm((L, D_MODEL, 6 * D_MODEL), 0.1 * D_MODEL ** -0.5),
        "ada_b": nrm((L, 6 * D_MODEL), 0.01),
        "w_in": nrm((L, D_MODEL, N_IN), D_MODEL ** -0.5),
        "hgrn_norm_g": 1.0 + nrm((L, W_HGRN), 0.02),
        "gla_alpha_up": nrm((L, GLA_GATE_RANK, W_GLA), GLA_GATE_RANK ** -0.5),
        "gla_alpha_b": nrm((L, W_GLA), 0.1),
        "gla_norm_g": 1.0 + nrm((L, W_GLA), 0.02),
        "rwkv_mu": jax.random.uniform(next(ks), (L, N_RWKV_COLS), jnp.float32, 0.0, 1.0),
        "rwkv_w0": jax.random.uniform(next(ks), (L, W_RWKV), jnp.float32, -6.0, 1.0),
        "rwkv_w_up": nrm((L, RWKV_DECAY_RANK, W_RWKV), 0.5 * RWKV_DECAY_RANK ** -0.5),
        "rwkv_a0": nrm((L, W_RWKV), 0.1),
        "rwkv_a_up": nrm((L, RWKV_ICLR_RANK, W_RWKV), RWKV_ICLR_RANK ** -0.5),
        "rwkv_g_up": nrm((L, RWKV_GATE_RANK, W_RWKV), RWKV_GATE_RANK ** -0.5),
        "rwkv_k_k": 0.85 + nrm((L, W_RWKV), 0.02),
        "rwkv_k_a": 1.0 + nrm((L, W_RWKV), 0.02),
        "rwkv_r_k": nrm((L, H_RWKV, HEAD_DIM), 0.1),
        "rwkv_gn_g": 1.0 + nrm((L, W_RWKV), 0.02),
        "rwkv_gn_b": nrm((L, W_RWKV), 0.02),
        "w_out": nrm((L, MIX_WIDTH, D_MODEL), DEEPNORM_BETA * MIX_WIDTH ** -0.5),
        "ln1_g": 1.0 + nrm((L, D_MODEL), 0.02),
        "ln1_b": nrm((L, D_MODEL), 0.02),
        "mlp_w_up": nrm((L, D_MODEL, D_FF), D_MODEL ** -0.5),
        "mlp_w_down": nrm((L, D_FF, D_MODEL), DEEPNORM_BETA * D_FF ** -0.5),
        "ln2_g": 1.0 + nrm((L, D_MODEL), 0.02),
        "ln2_b": nrm((L, D_MODEL), 0.02),
    }


def reference(x, c, hgrn_lb_logits, ada_w, ada_b, w_in, hgrn_norm_g, gla_alpha_up, gla_alpha_b,
              gla_norm_g, rwkv_mu, rwkv_w0, rwkv_w_up, rwkv_a0, rwkv_a_up, rwkv_g_up, rwkv_k_k,
              rwkv_k_a, rwkv_r_k, rwkv_gn_g, rwkv_gn_b, w_out, ln1_g, ln1_b, mlp_w_up, mlp_w_down,
              ln2_g, ln2_b):
    p = jax.nn.softmax(hgrn_lb_logits.astype(jnp.float32), axis=0)
    lower_bounds = jnp.cumsum(p, axis=0) - p[0:1]
    c_act = jax.nn.silu(c)
    for l in range(DEPTH):
        mod = c_act @ ada_w[l] + ada_b[l]
        shift1, scale1, gate1, shift2, scale2, gate2 = [m[:, None, :] for m in jnp.split(mod, 6, axis=-1)]

        h = x * (1.0 + scale1) + shift1
        z = h @ w_in[l]
        z_h, z_g, z_r = jnp.split(z, [N_HGRN_COLS, N_HGRN_COLS + N_GLA_COLS], axis=-1)
        o_h = hgrn2_mixer(z_h, lower_bounds[l], hgrn_norm_g[l])
        o_g = gla_mixer(z_g, gla_alpha_up[l], gla_alpha_b[l], gla_norm_g[l])
        o_r = rwkv7_mixer(z_r, rwkv_mu[l], rwkv_w0[l], rwkv_w_up[l], rwkv_a0[l], rwkv_a_up[l],
                          rwkv_g_up[l], rwkv_k_k[l], rwkv_k_a[l], rwkv_r_k[l], rwkv_gn_g[l], rwkv_gn_b[l])
        o = jnp.concatenate([o_h, o_g, o_r], axis=-1) @ w_out[l]
        x = layer_norm(DEEPNORM_ALPHA * x + (1.0 + gate1) * o, ln1_g[l], ln1_b[l])

        h = x * (1.0 + scale2) + shift2
        m = jnp.square(jax.nn.relu(h @ mlp_w_up[l])) @ mlp_w_down[l]
        x = layer_norm(DEEPNORM_ALPHA * x + (1.0 + gate2) * m, ln2_g[l], ln2_b[l])
    return x
```

```python
import contextlib
import numpy as np
import concourse.bass as bass
import concourse.mybir as mybir
from concourse.bass_utils import run_bass_kernel_spmd

F32 = mybir.dt.float32
BF16 = mybir.dt.bfloat16
AF = mybir.ActivationFunctionType
ALU = mybir.AluOpType

NCORES = 8
SEG = 2048
NBLK = 4
BLK = 512
D = 1024
KC = 8
DEPTH = 2
N_IN = 3984
ALPHA = (2.0 * DEPTH) ** 0.25
LN_EPS = 1e-5
RMS_EPS = 1e-5
GN_EPS = 64e-5
NDS = 12
FUSED = True
NSTREAM = 1
MB = 512 if NSTREAM == 1 else 256
NMB = SEG // MB
NCH = MB // 64
NTL = MB // 128

PAIRS = []
for hp in range(2):
    PAIRS.append(("h", hp, [("q", hp * 128, 128), ("f", 256 + hp * 128, 128),
                            ("i", 512 + hp * 128, 128), ("g", 768 + hp * 128, 128)]))
for gp in range(3):
    PAIRS.append(("g", gp, [("q", 1024 + gp * 128, 128), ("k", 1408 + gp * 128, 128),
                            ("v", 1792 + gp * 128, 128), ("r", 2176 + gp * 128, 128),
                            ("al", 2496, 128)]))
for rp in range(3):
    PAIRS.append(("r", rp, [("r", 2576 + rp * 128, 128), ("k", 2960 + rp * 128, 128),
                            ("v", 3344 + rp * 128, 128), ("wa", 3728, 128), ("gd", 3856, 128)]))
WPW = 640

PV = {}
_o = 0
for _n, _c in [("hng", 2), ("gab", 3), ("gng", 3), ("mu", 11), ("w0", 3), ("a0", 3), ("kk", 3),
               ("ka", 3), ("rk", 3), ("gg", 3), ("gb", 3), ("l1g", 8), ("l1b", 8), ("l2g", 8),
               ("l2b", 8), ("adab", 48), ("lbl", 4)]:
    PV[_n] = (_o, _c)
    _o += _c
NPV = _o


class Sched:
    def __init__(self, nc, sparse=None):
        self.nc = nc
        self.sparse = sparse
        self.waited = {e: set() for e in ("pe", "dve", "act", "pool", "sp")}
        self.phys = {e: 0 for e in ("pe", "dve", "act", "pool", "sp")}
        self.l2p = {e: {} for e in ("pe", "dve", "act", "pool", "sp")}
        self.E = {"pe": nc.tensor, "dve": nc.vector, "act": nc.scalar, "pool": nc.gpsimd, "sp": nc.sync}
        self.root = contextlib.ExitStack()
        self.scopes = [self.root]
        self.sem = {e: self.root.enter_context(nc.semaphore("s_" + e)) for e in self.E}
        self.cnt = {e: 0 for e in self.E}
        self.dsem = [self.root.enter_context(nc.semaphore("d%d" % i)) for i in range(NDS)]
        self.dcnt = [0] * NDS
        self.dnext = 0
        self.ccsem = self.root.enter_context(nc.semaphore("ccs"))
        self.cccnt = 0
        self.seen = {e: {} for e in self.E}
        self.lw = {}
        self.rd = {}
        self.ps_tiles = []
        self.ps_next = 0
        self.uid = 0
        self.mmsig = {}

    def sb(self, name, shape, dt):
        self.uid += 1
        return self.scopes[-1].enter_context(self.nc.sbuf_tensor("%s_%d" % (name, self.uid), list(shape), dt))

    @contextlib.contextmanager
    def scope(self):
        st = contextlib.ExitStack()
        self.scopes.append(st)
        try:
            yield
        finally:
            self.barrier()
            self.scopes.pop()
            st.close()

    def ps(self):
        t = self.ps_tiles[self.ps_next]
        self.ps_next = (self.ps_next + 1) % len(self.ps_tiles)
        return t

    def _semh(self, s):
        if s == "cc":
            return self.ccsem
        return self.dsem[s[1]] if isinstance(s, tuple) else self.sem[s]

    def _wait(self, e, clk):
        s, v, _ = clk
        if self.seen[e].get(s, 0) < v:
            pv = v
            if s in self.waited:
                self.waited[s].add(v)
                if self.sparse is not None:
                    pv = self.l2p[s][v]
            self.E[e].wait_ge(self._semh(s), pv)
            self.seen[e][s] = v

    @staticmethod
    def _key(a):
        return a.tensor.name

    def _deps(self, e, reads, writes, is_dma):
        for a in reads:
            for sname, (v, eng) in self.lw.get(self._key(a), {}).items():
                self._wait(e, (sname, v, eng))
        for a in writes:
            kx = self._key(a)
            for sname, (v, eng) in self.lw.get(kx, {}).items():
                if is_dma:
                    if eng is not None:
                        self._wait(e, (sname, v, eng))
                elif eng != e or e != "pe":
                    self._wait(e, (sname, v, eng))
            for sname, (v, eng) in self.rd.get(kx, {}).items():
                if is_dma or eng != e or e != "pe":
                    self._wait(e, (sname, v, eng))

    def _commit(self, clk, reads, writes, is_dma=False):
        for a in writes:
            kx = self._key(a)
            if is_dma:
                d = self.lw.setdefault(kx, {})
                for sname in [sn for sn, (v, eng) in d.items() if eng is not None]:
                    del d[sname]
                d[clk[0]] = (clk[1], clk[2])
            else:
                self.lw[kx] = {clk[0]: (clk[1], clk[2])}
            self.rd[kx] = {}
        for a in reads:
            kx = self._key(a)
            d = self.rd.setdefault(kx, {})
            if d.get(clk[0], (0, None))[0] < clk[1]:
                d[clk[0]] = (clk[1], clk[2])

    def op(self, e, fn, reads, writes):
        reads = [a for a in reads if isinstance(a, bass.AP)]
        self._deps(e, reads, writes, False)
        inst = fn()
        self.cnt[e] += 1
        if self.sparse is None or self.cnt[e] in self.sparse[e]:
            inst.then_inc(self.sem[e], 1)
            self.phys[e] += 1
            self.l2p[e][self.cnt[e]] = self.phys[e]
        self._commit((e, self.cnt[e], e), reads, writes)

    def dma(self, q, out, in_):
        i = self.dnext
        self.dnext = (self.dnext + 1) % NDS
        self._wait(q, (("d", i), self.dcnt[i], None))
        self._deps(q, [in_], [out], True)
        inst = self.E[q].dma_start(out=out, in_=in_)
        inst.then_inc(self.dsem[i], 16)
        self.dcnt[i] += 16
        self._commit((("d", i), self.dcnt[i], None), [in_], [out], True)

    def allgather(self, out_t, in_t):
        oa, ia = out_t.ap(), in_t.ap()
        self._deps("pool", [ia], [oa], True)
        inst = self.nc.gpsimd.collective_compute("AllGather", ALU.bypass, replica_groups=[list(range(NCORES))],
                                                 ins=[ia.opt()], outs=[oa.opt()])
        inst.then_inc(self.ccsem, 1)
        self.cccnt += 1
        self.E["pool"].wait_ge(self.ccsem, self.cccnt)
        self._commit(("cc", self.cccnt, None), [ia], [oa], True)

    def barrier(self):
        for e in self.E:
            for o in self.E:
                if o != e and self.cnt[o] > 0:
                    self._wait(e, (o, self.cnt[o], o))
            for i in range(NDS):
                if self.dcnt[i] > 0:
                    self._wait(e, (("d", i), self.dcnt[i], None))
            if self.cccnt > 0:
                self._wait(e, ("cc", self.cccnt, None))
        self.lw = {}
        self.rd = {}

    def mm(self, out, lhsT, rhs, start=True, stop=True):
        sig = (lhsT.start_partition(), lhsT.shape[0])
        kx = self._key(out)
        prev = self.mmsig.get(kx)
        if prev is not None and prev[0] != sig:
            self._wait("pe", ("pe", prev[1], "pe"))
        self.mmsig[kx] = (sig, self.cnt["pe"] + 1)
        self.op("pe", lambda: self.nc.tensor.matmul(out, lhsT, rhs, start=start, stop=stop,
                                                     skip_group_check=True), [lhsT, rhs], [out])

    def tr(self, out, in_, ident):
        self.op("pe", lambda: self.nc.tensor.transpose(out, in_, ident), [in_, ident], [out])

    def act(self, out, in_, func, bias=None, scale=None):
        kw = {}
        if bias is not None:
            kw["bias"] = bias
        if scale is not None:
            kw["scale"] = scale
        self.op("act", lambda: self.nc.scalar.activation(out, in_, func, **kw), [in_, bias, scale], [out])

    def tt(self, e, out, a, b, op):
        self.op(e, lambda: self.E[e].tensor_tensor(out, a, b, op), [a, b], [out])

    def ts(self, e, out, a, s1, op0, s2=None, op1=None):
        if op1 is None:
            self.op(e, lambda: self.E[e].tensor_scalar(out, a, s1, None, op0), [a, s1], [out])
        else:
            self.op(e, lambda: self.E[e].tensor_scalar(out, a, s1, s2, op0, op1), [a, s1, s2], [out])

    def stt(self, out, a, s, b, op0, op1):
        self.op("dve", lambda: self.nc.vector.scalar_tensor_tensor(out, a, s, b, op0, op1), [a, s, b], [out])

    def copy(self, e, out, in_):
        if e == "act":
            self.act(out, in_, AF.Copy)
        else:
            self.op(e, lambda: self.E[e].tensor_copy(out, in_), [in_], [out])

    def memset(self, e, ap, v):
        self.op(e, lambda: self.E[e].memset(ap, v), [], [ap])


def bc(ap, shape):
    return ap.to_broadcast(list(shape))


class Prog:
    def __init__(self, stages, dbg=None, sparse=None):
        self.sparse = sparse
        self.dbg = dbg or {}
        self.dumps = {}
        self.stg = []
        self.stgn = 0
        self.stages = stages
        self.fused = len(stages) == 4
        self.nc = bass.Bass("TRN2", target_bir_lowering=False)
        self.k = Sched(self.nc, self.sparse)
        self.build()

    def build(self):
        nc, k = self.nc, self.k
        dt = nc.dram_tensor
        self.d_xT = dt("xT", [128, KC, SEG + 1], F32, kind="ExternalInput").ap()
        self.d_cT = dt("cT", [128, KC, 2], F32, kind="ExternalInput").ap()
        self.d_cst = dt("cst", [128, 6, 128], F32, kind="ExternalInput").ap()
        self.d_msk = dt("msk", [128, 24], F32, kind="ExternalInput").ap()
        self.d_pv = dt("pv", [128, DEPTH, NPV], F32, kind="ExternalInput").ap()
        kinds = set(kd for kd, _ in self.stages)
        LD = [DEPTH] if self.fused else []
        self.dw = {}
        wl = [("ada_w", [D, 6 * D]), ("w_in", [D, N_IN]), ("gla_alpha_up", [16, 384]), ("rwkv_w_up", [64, 384]),
              ("rwkv_a_up", [64, 384]), ("rwkv_g_up", [128, 384])]
        if "B" in kinds:
            wl += [("w_out", [D, D]), ("mlp_w_up", [D, 4 * D]), ("mlp_w_down", [4 * D, D])]
        for nm, shp in wl:
            self.dw[nm] = dt(nm, LD + shp, F32, kind="ExternalInput").ap()
        if self.fused:
            self.t_st = dt("st_loc", [8 * 128, 128], F32)
            self.t_gall = dt("gall_int", [NCORES * 8 * 128, 128], F32)
            self.t_hl = dt("hl_loc", [128, KC], F32)
            self.t_hall = dt("hl_all", [NCORES * 128, KC], F32)
            self.d_st = self.t_st.ap()
            self.d_gall = self.t_gall.ap()
        else:
            self.d_gall = dt("gall", [NCORES * 8 * 128, 128], F32, kind="ExternalInput").ap()
            self.d_st = dt("st_out", [8 * 128, 128], F32, kind="ExternalOutput").ap()
        self.d_xo = dt("xT_out", [128, KC, SEG], F32, kind="ExternalOutput").ap()

        for i in range(8):
            k.ps_tiles.append(k.root.enter_context(nc.psum_tensor("ps%d" % i, [128, 512], F32)))

        self.xT = k.sb("xT", [128, KC, SEG + 1], F32)
        self.hT = k.sb("hT", [128, KC, SEG + 1], BF16)
        self.cst = k.sb("cst", [128, 6, 128], F32)
        self.msk = k.sb("msk", [128, 24], F32)
        self.pv = k.sb("pv", [128, DEPTH, NPV], F32)
        self.mod = k.sb("mod", [128, 48], F32)
        self.drv = k.sb("drv", [128, 64], F32)
        self.identb2 = k.sb("identb2", [128, 2, 128], BF16)
        self.mask4 = k.sb("mask4", [128, 4, 128], F32)
        self.maskL2 = k.sb("maskL2", [128, 2, 128], F32)
        self.ones5 = k.sb("ones5", [128, 512], F32)
        self.Hst = [k.sb("Hst%d" % i, [128, 64], F32) for i in range(8)]

        k.dma("sp", self.xT[:], self.d_xT)
        k.dma("sp", self.cst[:], self.d_cst)
        k.dma("sp", self.msk[:], self.d_msk)
        k.dma("sp", self.pv[:], self.d_pv)
        cst = self.cst
        self.ident = cst[:, 0, :]
        self.blockones = cst[:, 4, :]
        self.onesf = cst[:, 5, :]
        for h in range(2):
            k.copy("dve", self.identb2[:, h, :], cst[:, 0, :])
            k.copy("dve", self.mask4[:, 2 * h, :], cst[:, 1, :])
            k.copy("dve", self.mask4[:, 2 * h + 1, :], cst[:, 2, :])
            k.copy("dve", self.maskL2[:, h, :], cst[:, 3, :])
        k.memset("dve", self.ones5[:], 1.0)
        for i in range(8):
            k.memset("dve", self.Hst[i][:], 0.0)
        self._eps = {}
        for v in (RMS_EPS, GN_EPS):
            if v not in self._eps:
                t = k.sb("eps", [128, 1], F32)
                k.memset("dve", t[:], v)
                self._eps[v] = t

        for si, (kind, l) in enumerate(self.stages):
            if kind == "A":
                if not self.dbg.get("nosetup"):
                    self.setup_layer(l)
                if not self.dbg.get("nostage"):
                    self.stageA(l)
                if self.fused:
                    self.exchange_states()
            else:
                if not self.fused:
                    self.setup_layer(l)
                self.fold(l)
                if self.dbg.get("mixonly"):
                    self.make_hT(self.drv[:, 0:8], self.mod[:, 0:8])
                    with self.k.scope():
                        self.mixer(l, 2)
                else:
                    self.stageB(l)
                if self.fused and l == 0:
                    self.exchange_halo()
        last = self.stages[-1]
        if last[0] == "B":
            k.dma("sp", self.d_xo, self.xT[:, :, 1:SEG + 1])
        k.barrier()

    def dump(self, tag, ap):
        if not self.dbg.get("dump"):
            return
        if tag in self.dumps:
            return
        shp = list(ap.shape)
        d = self.nc.dram_tensor("dbg_" + tag, shp, F32, kind="ExternalOutput").ap()
        self.dumps[tag] = d
        self.k.dma("pool", d, ap)

    def load_cast(self, dst, src):
        k = self.k
        stg = self.stg[self.stgn % len(self.stg)]
        q = "sp" if self.stgn % 2 == 0 else "act"
        self.stgn += 1
        shp = list(dst.shape)
        p0 = dst.start_partition()
        n = 1
        for d_ in shp[1:]:
            n *= d_
        v = stg[p0:p0 + shp[0], 0:n]
        if len(shp) == 3:
            v = v.rearrange("p (a b) -> p a b", b=shp[2])
        k.dma(q, v, src)
        k.copy("pool", dst, v)

    def W(self, name, l):
        return self.dw[name][l] if self.fused else self.dw[name]

    def pvc(self, l, name, j=0, n=1):
        o, c = PV[name]
        return self.pv[:, l, o + j:o + j + n]

    def setup_layer(self, l):
        k = self.k
        with k.scope():
            ca = k.sb("cact", [128, KC, 2], F32)
            ct = k.sb("craw", [128, KC, 2], F32)
            k.dma("sp", ct[:], self.d_cT)
            k.act(ca[:], ct[:], AF.Silu)
            wb = [k.sb("adawb%d" % i, [128, KC, 512], F32) for i in range(2)]
            psm = k.ps()
            src = self.W("ada_w", l).rearrange("(kc p) n -> p kc n", p=128)
            for blk in range(12):
                w = wb[blk % 2]
                k.dma("sp" if blk % 2 == 0 else "act", w[:], src[:, :, blk * 512:(blk + 1) * 512])
                for j in range(4):
                    oc = blk * 4 + j
                    for kc in range(KC):
                        k.mm(psm[:, 2 * oc:2 * oc + 2], w[:, kc, j * 128:(j + 1) * 128], ca[:, kc, :],
                             start=(kc == 0), stop=(kc == KC - 1))
            o, _ = PV["adab"]
            pm = psm[:, 0:96].rearrange("p (o t) -> p o t", t=2)[:, :, 0]
            k.tt("dve", self.mod[:], pm, self.pv[:, l, o:o + 48], ALU.add)
            drv = self.drv
            k.ts("dve", drv[:, 0:8], self.mod[:, 8:16], 1.0, ALU.add)
            k.ts("dve", drv[:, 8:16], self.mod[:, 16:24], 1.0, ALU.add)
            k.ts("dve", drv[:, 16:24], self.mod[:, 32:40], 1.0, ALU.add)
            k.ts("dve", drv[:, 24:32], self.mod[:, 40:48], 1.0, ALU.add)
            o, _ = PV["lbl"]
            l0 = self.pv[:, 0, o:o + 2]
            l1 = self.pv[:, 0, o + 2:o + 4]
            tmp = k.sb("lbt", [128, 8], F32)
            k.tt("dve", tmp[:, 0:2], l0, l1, ALU.subtract)
            k.act(tmp[:, 2:4], tmp[:, 0:2], AF.Sigmoid)
            k.act(tmp[:, 4:6], tmp[:, 0:2], AF.Sigmoid, scale=-1.0)
            if l == 0:
                k.tt("dve", drv[:, 32:34], tmp[:, 2:4], tmp[:, 2:4], ALU.subtract)
            else:
                k.tt("dve", tmp[:, 6:8], tmp[:, 2:4], tmp[:, 4:6], ALU.add)
                k.tt("dve", drv[:, 32:34], tmp[:, 6:8], tmp[:, 2:4], ALU.subtract)
            k.ts("dve", drv[:, 34:36], drv[:, 32:34], -1.0, ALU.mult, 1.0, ALU.add)
            k.ts("dve", drv[:, 36:39], self.pvc(l, "ka", 0, 3), -1.0, ALU.mult, 1.0, ALU.add)

    def make_hT(self, sc, sh):
        k = self.k
        for kc in range(KC):
            k.act(self.hT[:, kc, :], self.xT[:, kc, :], AF.Identity, bias=sh[:, kc:kc + 1], scale=sc[:, kc:kc + 1])

    def stageA(self, l):
        k = self.k
        self.make_hT(self.drv[:, 0:8], self.mod[:, 0:8])
        with k.scope():
            self.mixer(l, 1)

    def stageB(self, l):
        k = self.k
        if not self.fused:
            self.make_hT(self.drv[:, 0:8], self.mod[:, 0:8])
        for kc in range(KC):
            k.act(self.xT[:, kc, 1:SEG + 1], self.xT[:, kc, 1:SEG + 1], AF.Copy, scale=ALPHA)
        with k.scope():
            self.mixer(l, 2)
        with k.scope():
            self.layer_norm(l, "l1g", "l1b")
        self.make_hT(self.drv[:, 16:24], self.mod[:, 24:32])
        for kc in range(KC):
            k.act(self.xT[:, kc, 1:SEG + 1], self.xT[:, kc, 1:SEG + 1], AF.Copy, scale=ALPHA)
        with k.scope():
            self.mlp(l)
        with k.scope():
            self.layer_norm(l, "l2g", "l2b")

    def layer_norm(self, l, gname, bname):
        k = self.k
        sq = [k.sb("lnsq%d" % i, [128, BLK], F32) for i in range(2)]
        M = k.sb("lnM", [128, BLK], F32)
        R = k.sb("lnR", [128, BLK], F32)
        T = [k.sb("lnT%d" % i, [128, BLK], F32) for i in range(2)]
        for b in range(NBLK):
            cs = slice(1 + b * BLK, 1 + (b + 1) * BLK)
            p1 = k.ps()
            for kc in range(KC):
                k.mm(p1[:], self.onesf, self.xT[:, kc, cs], start=(kc == 0), stop=(kc == KC - 1))
            p2 = k.ps()
            for kc in range(KC):
                s = sq[kc % 2]
                k.act(s[:], self.xT[:, kc, cs], AF.Square)
                k.mm(p2[:], self.onesf, s[:], start=(kc == 0), stop=(kc == KC - 1))
            k.act(M[:], p1[:], AF.Copy, scale=1.0 / D)
            k.tt("dve", R[:], M[:], M[:], ALU.mult)
            k.stt(R[:], p2[:], 1.0 / D, R[:], ALU.mult, ALU.subtract)
            k.ts("dve", R[:], R[:], LN_EPS, ALU.add)
            k.act(R[:], R[:], AF.Sqrt)
            k.op("dve", lambda: self.nc.vector.reciprocal(R[:], R[:]), [R[:]], [R[:]])
            for kc in range(KC):
                t = T[kc % 2]
                k.tt("dve", t[:], self.xT[:, kc, cs], M[:], ALU.subtract)
                k.tt("pool", t[:], t[:], R[:], ALU.mult)
                k.act(self.xT[:, kc, cs], t[:], AF.Identity, bias=self.pvc(l, bname, kc), scale=self.pvc(l, gname, kc))

    def mlp(self, l):
        k = self.k
        wup = [k.sb("wup%d" % i, [128, KC, 1024], BF16) for i in range(2)]
        wdn = [k.sb("wdn%d" % i, [128, 8, 1024], BF16) for i in range(2)]
        u = k.sb("u", [128, 8, BLK], BF16)
        rl = [k.sb("rl%d" % i, [128, BLK], F32) for i in range(2)]
        g2 = self.drv[:, 24:32]
        self.stg = [k.sb("stgm%d" % i, [128, 1024], F32) for i in range(3)]
        srcu = self.W("mlp_w_up", l).rearrange("(kc p) f -> p kc f", p=128)
        srcd = self.W("mlp_w_down", l).rearrange("(fc p) d -> p fc d", p=128)

        def load(fp):
            for kc in range(KC):
                self.load_cast(wup[fp % 2][:, kc, :], srcu[:, kc, fp * 1024:(fp + 1) * 1024])
            for fc in range(8):
                self.load_cast(wdn[fp % 2][:, fc, :], srcd[:, fp * 8 + fc, :])
        load(0)
        for fp in range(4):
            if fp + 1 < 4:
                load(fp + 1)
            wu, wd = wup[fp % 2], wdn[fp % 2]
            for b in range(NBLK):
                cs = slice(1 + b * BLK, 1 + (b + 1) * BLK)
                for fc in range(8):
                    p = k.ps()
                    for kc in range(KC):
                        k.mm(p[:], wu[:, kc, fc * 128:(fc + 1) * 128], self.hT[:, kc, cs], start=(kc == 0), stop=(kc == KC - 1))
                    r = rl[fc % 2]
                    k.act(r[:], p[:], AF.Relu)
                    k.tt("pool", u[:, fc, :], r[:], r[:], ALU.mult)
                for dc in range(KC):
                    p = k.ps()
                    for fc in range(8):
                        k.mm(p[:], wd[:, fc, dc * 128:(dc + 1) * 128], u[:, fc, :], start=(fc == 0), stop=(fc == 7))
                    k.stt(self.xT[:, dc, cs], p[:], g2[:, dc:dc + 1], self.xT[:, dc, cs], ALU.mult, ALU.add)

    def exchange_states(self):
        self.k.allgather(self.t_gall, self.t_st)

    def exchange_halo(self):
        k = self.k
        with k.scope():
            hs = k.sb("hsrc", [128, KC], F32)
            hl = k.sb("hl", [128, NCORES, KC], F32)
            acc = k.sb("hacc", [128, KC], F32)
            k.copy("dve", hs[:], self.xT[:, :, SEG])
            k.dma("sp", self.t_hl.ap(), hs[:])
            k.allgather(self.t_hall, self.t_hl)
            k.dma("sp", hl[:], self.t_hall.ap().rearrange("(e p) c -> p e c", p=128))
            k.memset("dve", acc[:], 0.0)
            for e in range(NCORES):
                k.stt(acc[:], hl[:, e, :], self.msk[:, 9 + e:10 + e], acc[:], ALU.mult, ALU.add)
            k.copy("dve", self.xT[:, :, 0], acc[:])

    def fold(self, l):
        k = self.k
        NS = 4
        with k.scope():
            G = [k.sb("fG%d" % i, [128, 128], F32) for i in range(NS)]
            Pbd = [k.sb("fP%d" % i, [128, 2, 64], F32) for i in range(NS)]
            PT = [k.sb("fPT%d" % i, [128, 128], F32) for i in range(NS)]
            tmp = [k.sb("ftmp%d" % i, [128, 64], F32) for i in range(NS)]
            for pi in range(8):
                k.memset("dve", self.Hst[pi][:], 0.0)
            bo2 = self.blockones.rearrange("p (h c) -> p h c", h=2)
            n = 0
            for e in range(NCORES):
                if e % 4 == 3:
                    continue
                for pi in range(8):
                    s_ = n % NS
                    n += 1
                    g, pb, pt, tm = G[s_], Pbd[s_], PT[s_], tmp[s_]
                    r0 = (e * 8 + pi) * 128
                    k.dma("sp", g[:], self.d_gall[r0:r0 + 128, :])
                    k.tt("dve", pb[:], bc(g[:, 64:128].unsqueeze(1), [128, 2, 64]), bo2, ALU.mult)
                    p = k.ps()
                    k.tr(p[:, 0:128], pb[:].rearrange("p h c -> p (h c)"), self.ident)
                    k.copy("act", pt[:], p[:, 0:128])
                    p2 = k.ps()
                    k.mm(p2[:, 0:64], pt[:], self.Hst[pi][:])
                    k.tt("dve", tm[:], p2[:, 0:64], g[:, 0:64], ALU.add)
                    k.tt("dve", tm[:], tm[:], self.Hst[pi][:], ALU.subtract)
                    k.stt(self.Hst[pi][:], tm[:], self.msk[:, 1 + e:2 + e], self.Hst[pi][:], ALU.mult, ALU.add)

    def alloc_stream(self, si, p2):
        k = self.k

        class S:
            pass
        S.si = si
        S.wp = k.sb("wp", [128, KC, WPW], BF16)
        if p2:
            S.wout = k.sb("wout", [128, D], BF16)
        S.Z = [k.sb("Z%d" % i, [128, MB + 1], F32) for i in range(5)]
        S.T = [k.sb("T%d" % i, [128, MB], F32) for i in range(5)]
        S.cumx = k.sb("cumx", [128, MB + 1], F32)
        S.Dt = k.sb("Dt", [128, MB], F32)
        S.Emp = k.sb("Emp", [128, MB], F32)
        S.Emn = k.sb("Emn", [128, MB], F32)
        S.gC = k.sb("gC", [128, NCH], F32)
        S.arT = k.sb("arT", [128, 2, MB], BF16)
        S.bT = k.sb("bT", [128, MB], BF16)
        S.kT = k.sb("kT", [128, MB], BF16)
        S.r0T = k.sb("r0T", [128, MB], BF16)
        S.X4 = k.sb("X4", [128, 4, MB], F32)
        S.yT = k.sb("yT", [128, MB], F32)
        S.oT = k.sb("oT", [128, MB], BF16)
        S.smb = k.sb("smb", [128, MB], BF16)
        S.Hf = k.sb("Hf", [128, 128], F32)
        k.memset("dve", S.Hf[:], 0.0)
        S.Hb = k.sb("Hb", [128, 128], BF16)
        S.TK = k.sb("TK", [128, 4, 128], BF16)
        S.SC = [k.sb("SC%d" % h, [128, 4, 128], BF16) for h in range(2)]
        S.NM = [k.sb("NM%d" % i, [128, 4, 128], BF16) for i in range(2)]
        S.TT = [k.sb("TT%d" % i, [128, 2, 128], BF16) for i in range(2)]
        S.AkV = k.sb("AkV", [128, 2, 64], BF16)
        S.Ut = k.sb("Ut", [128, 2, 128], F32)
        S.Ub = k.sb("Ub", [128, 2, 128], BF16)
        S.WT = k.sb("WT", [128, 128], BF16)
        k.memset("dve", S.Ut[:], 0.0)
        k.memset("dve", S.arT[:], 0.0)
        k.memset("dve", S.smb[:], 0.0)
        nb = 8 // NSTREAM
        S.banks = [k.ps_tiles[si * nb + i] for i in range(nb - 1)]
        S.bankY = k.ps_tiles[si * nb + nb - 1]
        S.bn = 0
        return S

    def mixer(self, l, phase):
        k = self.k
        p2 = phase == 2

        class Sh:
            pass
        Sh.gau = k.sb("gau", [128, 384], BF16)
        k.memset("dve", Sh.gau[:], 0.0)
        Sh.wau = k.sb("wau", [128, 384], BF16)
        Sh.gu = k.sb("gu", [128, 384], BF16)
        self.stg = [k.sb("stg%d" % i, [128, 1024], F32) for i in range(2)]
        self.load_cast(Sh.gau[64:80, :], self.W("gla_alpha_up", l))
        self.load_cast(Sh.wau[0:64, :], self.W("rwkv_w_up", l))
        self.load_cast(Sh.wau[64:128, :], self.W("rwkv_a_up", l))
        self.load_cast(Sh.gu[:], self.W("rwkv_g_up", l))
        streams = [self.alloc_stream(si, p2) for si in range(NSTREAM)]
        queue = [q for q in [5, 6, 7, 2, 3, 4, 0, 1] if q in self.dbg.get("pairs", range(8))]
        gens = [None] * NSTREAM
        while True:
            alive = False
            for si in range(NSTREAM):
                if gens[si] is None and queue:
                    gens[si] = self.pair_stream(l, phase, queue.pop(0), streams[si], Sh)
                if gens[si] is not None:
                    alive = True
                    try:
                        next(gens[si])
                    except StopIteration:
                        gens[si] = None
            if not alive:
                break

    def pair_stream(self, l, phase, pi, S, Sh):
        k, nc = self.k, self.nc
        p2 = phase == 2
        VW = 64 if p2 else 128
        ptype, pidx, cols = PAIRS[pi]
        rw = ptype == "r"
        g1 = self.drv[:, 8:16]
        src = self.W("w_in", l).rearrange("(kc p) n -> p kc n", p=128)
        w = S.wp
        offs = {}
        off = 0
        for (nm, c0, wd_) in cols:
            self.load_cast(w[:, :, off:off + wd_], src[:, :, c0:c0 + wd_])
            offs[nm] = (off, wd_)
            off += wd_
        if p2:
            self.load_cast(S.wout[:], self.W("w_out", l)[pi * 128:(pi + 1) * 128, :])
        qs = 1.0 if rw else 0.125
        Z, T, cumx, Dt, Emp, Emn, gC = S.Z, S.T, S.cumx, S.Dt, S.Emp, S.Emn, S.gC
        arT, bT, kT, r0T, X4, yT, oT, smb = S.arT, S.bT, S.kT, S.r0T, S.X4, S.yT, S.oT, S.smb
        Hf, Hb, TK, SC, NM, TT, AkV, Ut, Ub, WT = S.Hf, S.Hb, S.TK, S.SC, S.NM, S.TT, S.AkV, S.Ut, S.Ub, S.WT
        gau, wau, gu = Sh.gau, Sh.wau, Sh.gu

        def ps():
            t = S.banks[S.bn]
            S.bn = (S.bn + 1) % len(S.banks)
            return t

        def proj(nm, cs, ncol=MB):
            o, wd_ = offs[nm]
            p = ps()
            for kc in range(KC):
                k.mm(p[0:wd_, 0:ncol], w[:, kc, o:o + wd_], self.hT[:, kc, cs], start=(kc == 0), stop=(kc == KC - 1))
            return p

        yield
        if p2:
            k.copy("dve", Hf[:, 0:64], self.Hst[pi][:])
        else:
            k.memset("dve", Hf[:, 0:64], 0.0)
            k.tt("dve", Hf[:, 64:128], self.ident[:, 0:64], self.ident[:, 64:128], ALU.add)
        k.copy("pool", Hb[:], Hf[:])
        k.memset("dve", cumx[:, 0:1], 0.0)
        if rw:
            zt = {"r": 0, "k": 1, "v": 2, "wa": 3, "gd": 4}
            mui = {"r": pidx, "k": 3 + pidx, "v": 6 + pidx, "wa": 9, "gd": 10}
            need = ["r", "k", "v", "wa", "gd"] if p2 else ["k", "v", "wa"]
            for nm in need:
                p = proj(nm, slice(0, 2), 2)
                k.ts("dve", Z[zt[nm]][:, 0:1], p[:, 0:1], self.msk[:, 0:1], ALU.mult)
                yield

        for b in range(NMB):
            cs = slice(1 + b * MB, 1 + (b + 1) * MB)
            AH, KK, AA, BB, LW = T
            BON = KK[:]
            R = Z[0][:, 1:MB + 1]
            K2 = Z[1][:, 1:MB + 1]
            G = Z[4][:, 1:MB + 1]
            V = X4[:, 3, :]
            if ptype == "h":
                lb = self.drv[:, 32 + pidx:33 + pidx]
                oml = self.drv[:, 34 + pidx:35 + pidx]
                if p2:
                    p = proj("q", cs)
                    k.act(R, p[:, 0:MB], AF.Silu)
                    yield
                    p = proj("g", cs)
                    k.act(G, p[:, 0:MB], AF.Silu)
                    yield
                p = proj("f", cs)
                k.act(AH[:], p[:, 0:MB], AF.Sigmoid)
                k.act(K2, p[:, 0:MB], AF.Sigmoid, scale=-1.0)
                yield
                k.ts("dve", AH[:], AH[:], oml, ALU.mult, lb, ALU.add)
                k.ts("dve", AH[:], AH[:], 1e-30, ALU.max)
                k.act(LW[:], AH[:], AF.Ln)
                k.ts("dve", K2, K2, oml, ALU.mult)
                yield
                p = proj("i", cs)
                k.copy("act", V, p[:, 0:MB])
                yield
            elif ptype == "g":
                if p2:
                    p = proj("q", cs)
                    k.copy("act", R, p[:, 0:MB])
                    yield
                    p = proj("r", cs)
                    k.act(G, p[:, 0:MB], AF.Silu)
                    yield
                p = proj("k", cs)
                k.copy("act", K2, p[:, 0:MB])
                yield
                p = proj("v", cs)
                k.copy("act", V, p[:, 0:MB])
                yield
                p = proj("al", cs)
                k.copy("act", smb[64:128, :], p[64:128, 0:MB])
                yield
                p = ps()
                k.mm(p[:, 0:MB], gau[64:128, pidx * 128:(pidx + 1) * 128], smb[64:128, :])
                k.act(AH[:], p[:, 0:MB], AF.Identity, bias=self.pvc(l, "gab", pidx))
                yield
                k.stt(KK[:], AH[:], -1.0, AH[:], ALU.mult, ALU.max)
                k.act(KK[:], KK[:], AF.Exp, scale=-1.0)
                yield
                k.ts("dve", KK[:], KK[:], 1.0, ALU.add)
                k.act(KK[:], KK[:], AF.Ln)
                k.ts("dve", AH[:], AH[:], 0.0, ALU.min, 1.0 / 16, ALU.mult)
                yield
                k.stt(LW[:], KK[:], -1.0 / 16, AH[:], ALU.mult, ALU.add)
                yield
            else:
                for nm in need:
                    z = Z[zt[nm]]
                    p = proj(nm, cs)
                    k.copy("act", z[:, 1:MB + 1], p[:, 0:MB])
                    yield
                    k.tt("pool", Dt[:], z[:, 0:MB], z[:, 1:MB + 1], ALU.subtract)
                    k.copy("pool", z[:, 0:1], z[:, MB:MB + 1])
                    k.stt(z[:, 1:MB + 1], Dt[:], self.pvc(l, "mu", mui[nm]), z[:, 1:MB + 1], ALU.mult, ALU.add)
                    yield
                Zr, Zk, Zv, Zw, Zg = [z[:, 1:MB + 1] for z in Z]
                k.copy("pool", V, Zv)
                k.act(smb[0:64, :], Zw[0:64, :], AF.Tanh)
                k.copy("act", smb[64:128, :], Zw[64:128, :])
                yield
                p = ps()
                k.mm(p[:, 0:MB], wau[0:64, pidx * 128:(pidx + 1) * 128], smb[0:64, :])
                k.act(LW[:], p[:, 0:MB], AF.Sigmoid, bias=self.pvc(l, "w0", pidx))
                k.ts("dve", LW[:], LW[:], -0.6065306597126334, ALU.mult)
                yield
                p = ps()
                k.mm(p[:, 0:MB], wau[64:128, pidx * 128:(pidx + 1) * 128], smb[64:128, :])
                k.act(AH[:], p[:, 0:MB], AF.Sigmoid, bias=self.pvc(l, "a0", pidx))
                yield
                k.ts("dve", KK[:], Zk, self.pvc(l, "kk", pidx), ALU.mult)
                k.act(Dt[:], KK[:], AF.Square)
                yield
                p = ps()
                k.mm(p[:, 0:MB], self.blockones, Dt[:])
                k.act(Dt[:], p[:, 0:MB], AF.Sqrt)
                yield
                k.ts("dve", Dt[:], Dt[:], 1e-12, ALU.max)
                k.op("dve", lambda: nc.vector.reciprocal(Dt[:], Dt[:]), [Dt[:]], [Dt[:]])
                k.tt("dve", KK[:], KK[:], Dt[:], ALU.mult)
                yield
                k.ts("dve", Dt[:], AH[:], self.pvc(l, "ka", pidx), ALU.mult, self.drv[:, 36 + pidx:37 + pidx], ALU.add)
                k.tt("dve", K2, Zk, Dt[:], ALU.mult)
                k.act(Dt[:], LW[:], AF.Exp, scale=-1.0)
                yield
                k.stt(AA[:], KK[:], -1.0, Dt[:], ALU.mult, ALU.mult)
                k.tt("pool", BB[:], KK[:], AH[:], ALU.mult)
                yield
                if p2:
                    k.act(smb[:], Zg, AF.Sigmoid)
                    p = ps()
                    k.mm(p[:, 0:MB], gu[:, pidx * 128:(pidx + 1) * 128], smb[:])
                    k.copy("act", G, p[:, 0:MB])
                    yield
                    k.stt(Dt[:], R, self.pvc(l, "rk", pidx), K2, ALU.mult, ALU.mult)
                    p = ps()
                    k.mm(p[:, 0:MB], self.blockones, Dt[:])
                    k.tt("dve", BON, p[:, 0:MB], V, ALU.mult)
                    yield

            k.op("dve", lambda: nc.vector.tensor_tensor_scan(cumx[:, 1:MB + 1], self.ones5[:, 0:MB], LW[:], cumx[:, 0:1],
                                                             ALU.mult, ALU.add),
                 [self.ones5[:], LW[:], cumx[:]], [cumx[:]])
            c3 = cumx[:, 1:MB + 1].rearrange("p (c t) -> p c t", t=64)
            cv = cumx[:, 0:MB].rearrange("p (c t) -> p c t", t=64)
            r_start = cv[:, :, 0:1]
            r_mid = cv[:, :, 32:33]
            r_end = c3[:, :, 63:64]

            def d3(t):
                return t[:].rearrange("p (c t) -> p c t", t=64)
            k.tt("dve", d3(Dt), c3, bc(r_mid, [128, NCH, 64]), ALU.subtract)
            k.act(Emp[:], Dt[:], AF.Exp)
            k.act(Emn[:], Dt[:], AF.Exp, scale=-1.0)
            yield
            k.tt("pool", kT[:], K2, Emn[:], ALU.mult)
            if p2:
                k.stt(arT[:, 1, :], R, qs, Emp[:], ALU.mult, ALU.mult)
            if rw:
                k.tt("dve", arT[:, 0, :], AA[:], Emp[:], ALU.mult)
                k.tt("pool", bT[:], BB[:], Emn[:], ALU.mult)
            yield
            k.tt("dve", d3(Dt), c3, bc(r_start, [128, NCH, 64]), ALU.subtract)
            k.act(Emp[:], Dt[:], AF.Exp)
            k.tt("dve", d3(Dt), c3, bc(r_end, [128, NCH, 64]), ALU.subtract)
            k.act(Emn[:], Dt[:], AF.Exp, scale=-1.0)
            yield
            k.tt("dve", gC[:].unsqueeze(2), r_end, r_start, ALU.subtract)
            k.act(gC[:], gC[:], AF.Exp)
            k.copy("pool", cumx[:, 0:1], cumx[:, MB:MB + 1])
            k.tt("pool", X4[:, 2, :], K2, Emn[:], ALU.mult)
            if p2:
                k.stt(r0T[:], R, qs, Emp[:], ALU.mult, ALU.mult)
            if rw:
                k.tt("dve", X4[:, 0, :], AA[:], Emp[:], ALU.mult)
                k.tt("pool", X4[:, 1, :], BB[:], Emn[:], ALU.mult)
            yield

            for tt_ in range(NTL):
                ts_ = slice(tt_ * 128, (tt_ + 1) * 128)
                p = ps()
                js = [0, 1, 2, 3] if rw else [2, 3]
                for j in js:
                    k.tr(p[:, j * 128:(j + 1) * 128], X4[:, j, ts_], self.ident)
                k.copy("act", TK[:, js[0]:4, :], p[:, js[0] * 128:512].rearrange("p (j c) -> p j c", c=128))
                yield
                if rw:
                    pN = ps()
                    for h in range(2):
                        hs = slice(h * 64, (h + 1) * 64)
                        p = ps()
                        if p2:
                            rhs = arT[hs, :, ts_]
                            k.mm(p[:, 0:256].rearrange("p (j c) -> p j c", c=128), bT[hs, ts_], rhs)
                            k.mm(p[:, 256:512].rearrange("p (j c) -> p j c", c=128), kT[hs, ts_], rhs)
                        else:
                            k.mm(p[:, 0:128], bT[hs, ts_], arT[hs, 0, ts_])
                            k.mm(p[:, 256:384], kT[hs, ts_], arT[hs, 0, ts_])
                            k.mm(p[:, 128:256], bT[hs, ts_], arT[hs, 0, ts_])
                            k.mm(p[:, 384:512], kT[hs, ts_], arT[hs, 0, ts_])
                        k.tt("dve", SC[h][:], p[:].rearrange("p (j c) -> p j c", c=128), self.mask4[:], ALU.mult)
                        k.mm(pN[:, h * 128:(h + 1) * 128], arT[hs, 0, ts_], bT[hs, ts_])
                        yield
                    nm_ = NM[0]
                    k.tt("dve", nm_[:, 0:2, :], pN[:, 0:256].rearrange("p (j c) -> p j c", c=128), self.maskL2[:], ALU.mult)
                    for h in range(2):
                        k.copy("pool", nm_[:, 2 + h, :], SC[h][:, 0, :])
                    tcur = TT[0]
                    k.tt("pool", tcur[:], self.identb2[:], nm_[:, 2:4, :], ALU.add)
                    yield
                    for lev in range(1, 6):
                        nprev, nnew = NM[(lev - 1) % 2], NM[lev % 2]
                        p = ps()
                        for h in range(2):
                            k.mm(p[:, h * 128:(h + 1) * 128], nprev[:, 2 + h, :], nprev[:, h, :])
                        if lev < 5:
                            for h in range(2):
                                k.mm(p[:, (2 + h) * 128:(3 + h) * 128], nprev[:, h, :], nprev[:, 2 + h, :])
                            k.copy("act", nnew[:], p[:].rearrange("p (j c) -> p j c", c=128))
                        else:
                            k.copy("act", nnew[:, 0:2, :], p[:, 0:256].rearrange("p (j c) -> p j c", c=128))
                        yield
                        pc = ps()
                        for h in range(2):
                            k.mm(pc[:, h * 128:(h + 1) * 128], nnew[:, h, :], tcur[:, h, :])
                        tnew = TT[lev % 2]
                        k.tt("dve", tnew[:], pc[:, 0:256].rearrange("p (j c) -> p j c", c=128), tcur[:], ALU.add)
                        tcur = tnew
                        yield
                    p = ps()
                    for h in range(2):
                        k.mm(p[:, h * 64:(h + 1) * 64], SC[h][:, 2, :], TK[:, 3, h * 64:(h + 1) * 64])
                    k.copy("act", AkV[:], p[:, 0:128].rearrange("p (h c) -> p h c", c=64))
                    yield
                    p = ps()
                    for h in range(2):
                        k.mm(p[:, h * 64:(h + 1) * 64], tcur[:, h, :], AkV[:, h, :])
                    k.copy("act", Ut[:, :, 0:64], p[:, 0:128].rearrange("p (h c) -> p h c", c=64))
                    p = ps()
                    for h in range(2):
                        k.mm(p[h * 64:(h + 1) * 64, 0:128], TK[:, 0, h * 64:(h + 1) * 64], tcur[:, h, :])
                    k.copy("act", WT[:], p[:, 0:128])
                    yield
                elif p2:
                    p = ps()
                    for h in range(2):
                        hs = slice(h * 64, (h + 1) * 64)
                        k.mm(p[:, h * 128:(h + 1) * 128], kT[hs, ts_], arT[hs, 1, ts_])
                    for h in range(2):
                        k.tt("dve", SC[h][:, 3, :], p[:, h * 128:(h + 1) * 128], self.cst[:, 2, :], ALU.mult)
                    yield
                pY = S.bankY
                for c in range(2):
                    cr = slice(c * 64, (c + 1) * 64)
                    ci = tt_ * 2 + c
                    bcs = slice(tt_ * 128 + c * 64, tt_ * 128 + (c + 1) * 64)
                    ycs = slice(c * 64, (c + 1) * 64)
                    if rw:
                        pU = ps()
                        for h in range(2):
                            hs = slice(h * 64, (h + 1) * 64)
                            k.mm(pU[cr, h * 128:h * 128 + VW], WT[hs, cr], Hb[hs, 0:VW])
                        k.tt("dve", Ub[cr, :, 0:VW], pU[cr, 0:256].rearrange("p (h c) -> p h c", c=128)[:, :, 0:VW],
                             Ut[cr, :, 0:VW], ALU.add)
                        yield
                    if p2:
                        for h in range(2):
                            hs = slice(h * 64, (h + 1) * 64)
                            k.mm(pY[hs, ycs], Hb[hs, 0:64], r0T[hs, bcs], start=True, stop=False)
                            if rw:
                                k.mm(pY[hs, ycs], Ub[cr, h, 0:64], SC[h][cr, 1, cr], start=False, stop=False)
                            k.mm(pY[hs, ycs], TK[cr, 3, h * 64:(h + 1) * 64], SC[h][cr, 3, cr], start=False, stop=True)
                    pH = ps()
                    for h in range(2):
                        hs = slice(h * 64, (h + 1) * 64)
                        if rw:
                            k.mm(pH[hs, 0:64], TK[cr, 1, h * 64:(h + 1) * 64], Ub[cr, h, 0:64], start=True, stop=False)
                            k.mm(pH[hs, 0:64], TK[cr, 2, h * 64:(h + 1) * 64], TK[cr, 3, h * 64:(h + 1) * 64], start=False, stop=True)
                            if not p2:
                                k.mm(pH[hs, 64:128], TK[cr, 1, h * 64:(h + 1) * 64], Ub[cr, h, 64:128])
                        else:
                            k.mm(pH[hs, 0:64], TK[cr, 2, h * 64:(h + 1) * 64], TK[cr, 3, h * 64:(h + 1) * 64])
                    wdt = VW if rw else 64
                    k.stt(Hf[:, 0:wdt], Hf[:, 0:wdt], gC[:, ci:ci + 1], pH[:, 0:wdt], ALU.mult, ALU.add)
                    if (not rw) and (not p2):
                        k.ts("dve", Hf[:, 64:128], Hf[:, 64:128], gC[:, ci:ci + 1], ALU.mult)
                    k.copy("act", Hb[:], Hf[:])
                    yield
                if p2:
                    k.copy("act", yT[:, ts_], pY[:, 0:128])
                    yield

            if p2:
                if not rw:
                    k.act(Dt[:], yT[:], AF.Square)
                    p = ps()
                    k.mm(p[:, 0:MB], self.blockones, Dt[:])
                    k.act(Dt[:], p[:, 0:MB], AF.Sqrt, bias=self.epsc(RMS_EPS), scale=1.0 / 64)
                    yield
                    k.op("dve", lambda: nc.vector.reciprocal(Dt[:], Dt[:]), [Dt[:]], [Dt[:]])
                    k.tt("dve", Dt[:], yT[:], Dt[:], ALU.mult)
                    gname = "hng" if ptype == "h" else "gng"
                    k.stt(oT[:], Dt[:], self.pvc(l, gname, pidx), G, ALU.mult, ALU.mult)
                    yield
                else:
                    p = ps()
                    k.mm(p[:, 0:MB], self.blockones, yT[:])
                    k.stt(yT[:], p[:, 0:MB], -1.0 / 64, yT[:], ALU.mult, ALU.add)
                    k.act(Dt[:], yT[:], AF.Square)
                    yield
                    p = ps()
                    k.mm(p[:, 0:MB], self.blockones, Dt[:])
                    k.act(Dt[:], p[:, 0:MB], AF.Sqrt, bias=self.epsc(GN_EPS), scale=1.0 / 64)
                    yield
                    k.op("dve", lambda: nc.vector.reciprocal(Dt[:], Dt[:]), [Dt[:]], [Dt[:]])
                    k.tt("dve", Dt[:], yT[:], Dt[:], ALU.mult)
                    k.ts("dve", Dt[:], Dt[:], self.pvc(l, "gg", pidx), ALU.mult, self.pvc(l, "gb", pidx), ALU.add)
                    yield
                    k.tt("dve", Dt[:], Dt[:], BON, ALU.add)
                    k.tt("dve", oT[:], Dt[:], G, ALU.mult)
                    yield
                for dc in range(KC):
                    p = ps()
                    k.mm(p[:, 0:MB], S.wout[:, dc * 128:(dc + 1) * 128], oT[:])
                    k.stt(self.xT[:, dc, cs], p[:, 0:MB], g1[:, dc:dc + 1], self.xT[:, dc, cs], ALU.mult, ALU.add)
                    yield

        if not p2:
            k.dma("sp", self.d_st[pi * 128:(pi + 1) * 128, :], Hf[:])
        yield

    def epsc(self, v):
        return self._eps[v][:, 0:1]


def _consts():
    c = np.zeros((128, 6, 128), np.float32)
    i = np.arange(128)
    same = (i[:, None] // 64) == (i[None, :] // 64)
    c[:, 0, :] = np.eye(128)
    c[:, 1, :] = (same & (i[:, None] < i[None, :]))
    c[:, 2, :] = (same & (i[:, None] <= i[None, :]))
    c[:, 3, :] = (same & (i[:, None] > i[None, :]))
    c[:, 4, :] = same
    c[:, 5, :] = 1.0
    return c


def _fm(v):
    v = np.asarray(v, np.float32).reshape(-1)
    return v.reshape(-1, 128).T


def _pack_pv(inp):
    pv = np.zeros((128, DEPTH, NPV), np.float32)
    names = {"hng": "hgrn_norm_g", "gab": "gla_alpha_b", "gng": "gla_norm_g", "mu": "rwkv_mu", "w0": "rwkv_w0",
             "a0": "rwkv_a0", "kk": "rwkv_k_k", "ka": "rwkv_k_a", "rk": "rwkv_r_k", "gg": "rwkv_gn_g",
             "gb": "rwkv_gn_b", "l1g": "ln1_g", "l1b": "ln1_b", "l2g": "ln2_g", "l2b": "ln2_b", "adab": "ada_b"}
    for l in range(DEPTH):
        for kx, nm in names.items():
            o, c = PV[kx]
            pv[:, l, o:o + c] = _fm(inp[nm][l])
        o, c = PV["lbl"]
        pv[:, l, o:o + 2] = _fm(inp["hgrn_lb_logits"][0])
        pv[:, l, o + 2:o + 4] = _fm(inp["hgrn_lb_logits"][1])
    return pv


_PROGS = {}


def _prog(key):
    if key not in _PROGS:
        first = Prog(list(key))
        _PROGS[key] = Prog(list(key), sparse=first.k.waited)
    return _PROGS[key]


def _xT_maps(xfull_T_list):
    out = []
    for i in range(NCORES):
        b, j = i // 4, i % 4
        xt = np.zeros((D, SEG + 1), np.float32)
        xt[:, 1:] = xfull_T_list[b][:, j * SEG:(j + 1) * SEG]
        if j > 0:
            xt[:, 0] = xfull_T_list[b][:, j * SEG - 1]
        out.append(np.ascontiguousarray(xt.reshape(KC, 128, SEG + 1).transpose(1, 0, 2)))
    return out


def kernel(**inp):
    inp = {k_: np.asarray(v) for k_, v in inp.items()}
    x = inp["x"].astype(np.float32, copy=False)
    cst = _consts()
    pv = _pack_pv(inp)
    base = []
    for i in range(NCORES):
        b, j = i // 4, i % 4
        msk = np.zeros((128, 24), np.float32)
        msk[:, 0] = 0.0 if j == 0 else 1.0
        for e in range(NCORES):
            if e // 4 == b and e % 4 < j:
                msk[:, 1 + e] = 1.0
            if j > 0 and e == i - 1:
                msk[:, 9 + e] = 1.0
        cT = np.repeat(_fm(inp["c"][b])[:, :, None], 2, axis=2)
        m = {"cT": np.ascontiguousarray(cT), "cst": cst, "msk": msk, "pv": pv}
        base.append(m)
    xT_full = [np.ascontiguousarray(x[b].T) for b in range(2)]
    if FUSED:
        xs = _xT_maps(xT_full)
        wn = ["ada_w", "w_in", "gla_alpha_up", "rwkv_w_up", "rwkv_a_up", "rwkv_g_up", "w_out", "mlp_w_up", "mlp_w_down"]
        wd = {n_: np.ascontiguousarray(inp[n_]) for n_ in wn}
        maps = [dict(base[i], xT=xs[i], **wd) for i in range(NCORES)]
        res = run_bass_kernel_spmd(_prog((("A", 0), ("B", 0), ("A", 1), ("B", 1))).nc, maps, core_ids=list(range(NCORES)))
        for b in range(2):
            xT_full[b] = np.concatenate(
                [np.asarray(res.results[b * 4 + j]["xT_out"]).transpose(1, 0, 2).reshape(D, SEG) for j in range(4)], axis=1)
        out = np.stack([xT_full[b].T for b in range(2)], axis=0)
        return np.ascontiguousarray(out.astype(np.float32))
    gall = np.zeros((NCORES * 8 * 128, 128), np.float32)
    for (kind, l) in [("A", 0), ("B", 0), ("A", 1), ("B", 1)]:
        xs = _xT_maps(xT_full)
        wn = ["ada_w", "w_in", "gla_alpha_up", "rwkv_w_up", "rwkv_a_up", "rwkv_g_up"]
        if kind == "B":
            wn += ["w_out", "mlp_w_up", "mlp_w_down"]
        wd = {n_: np.ascontiguousarray(inp[n_][l]) for n_ in wn}
        maps = [dict(base[i], xT=xs[i], gall=gall, **wd) for i in range(NCORES)]
        res = run_bass_kernel_spmd(_prog(((kind, l),)).nc, maps, core_ids=list(range(NCORES)))
        if kind == "A":
            gall = np.ascontiguousarray(np.concatenate([np.asarray(r["st_out"]) for r in res.results], axis=0))
        else:
            for b in range(2):
                xT_full[b] = np.concatenate(
                    [np.asarray(res.results[b * 4 + j]["xT_out"]).transpose(1, 0, 2).reshape(D, SEG) for j in range(4)], axis=1)
    out = np.stack([xT_full[b].T for b in range(2)], axis=0)
    return np.ascontiguousarray(out.astype(np.float32))
```
